# Optimizing a Trainium2 kernel written in Bass

```python
import jax, jax.numpy as jnp
from jax import lax
import numpy as np


D_MODEL = 1024
BATCH = 4
SEQ = 4096
DEPTH = 1

CHUNK = 64
MIX_WIDTH = D_MODEL
RET_HEAD_DIM = D_MODEL // 8
RET_HEADS = 4
RET_WIDTH = RET_HEADS * RET_HEAD_DIM
CONV_WIDTH = MIX_WIDTH - RET_WIDTH
CONV_K = 3
D_FF = ((8 * D_MODEL + 3 * 256 - 1) // (3 * 256)) * 256
IN_COLS = 4 * RET_WIDTH + 3 * CONV_WIDTH
ROPE_BASE = 10000.0
NORM_EPS = 1e-6

kernel_name = 'hybrid_retention_shortconv_block'


def rmsnorm(x, g):
    xf = x.astype(jnp.float32)
    y = xf * lax.rsqrt(jnp.mean(xf * xf, axis=-1, keepdims=True) + NORM_EPS)
    return (y * g.astype(jnp.float32)).astype(x.dtype)


def rotary(t, pos):
    d = t.shape[-1]
    inv_freq = 1.0 / (ROPE_BASE ** (jnp.arange(0, d, 2, dtype=jnp.float32) / d))
    ang = pos[:, None] * inv_freq[None, :]
    cos = jnp.cos(ang)[None, :, None, :]
    sin = jnp.sin(ang)[None, :, None, :]
    t1, t2 = t[..., : d // 2], t[..., d // 2:]
    return jnp.concatenate([t1 * cos - t2 * sin, t1 * sin + t2 * cos], axis=-1)


def chunk_retention(q, k, v):
    b, s, h, dk = q.shape
    dv = v.shape[-1]
    nc = s // CHUNK
    log_g = jnp.log1p(-jnp.exp2(-5.0 - jnp.arange(h, dtype=jnp.float32)))
    idx = jnp.arange(CHUNK, dtype=jnp.float32)
    dist = jnp.abs(idx[:, None] - idx[None, :])
    decay_intra = jnp.exp(log_g[:, None, None] * dist)
    xi = jnp.exp(log_g[:, None] * (idx[None, :] + 1.0))
    zeta = jnp.exp(log_g[:, None] * (CHUNK - 1.0 - idx[None, :]))
    chunk_decay = jnp.exp(log_g * CHUNK)

    qc = q.reshape(b, nc, CHUNK, h, dk)
    kc = k.reshape(b, nc, CHUNK, h, dk)
    vc = v.reshape(b, nc, CHUNK, h, dv)

    scores = jnp.einsum('bnihd,bnjhd->bnhij', qc, kc) * decay_intra[None, None]
    intra = jnp.einsum('bnhij,bnjhe->bnihe', scores, vc)

    incr = jnp.einsum('bnjhd,hj,bnjhe->nbhde', kc, zeta, vc)

    def step(state, u):
        return chunk_decay[None, :, None, None] * state + u, state

    _, state_prev = lax.scan(step, jnp.zeros((b, h, dk, dv), jnp.float32), incr)
    cross = jnp.einsum('bnihd,hi,nbhde->bnihe', qc, xi, state_prev)
    return (intra + cross).reshape(b, s, h, dv)


def causal_depthwise_conv(u, w):
    s = u.shape[1]
    up = jnp.pad(u, ((0, 0), (CONV_K - 1, 0), (0, 0)))
    out = w[0] * up[:, 0:s]
    for tap in range(1, CONV_K):
        out = out + w[tap] * up[:, tap:tap + s]
    return out


def setup_inputs(seed: int = 0) -> dict:
    key = jax.random.key(seed)
    ks = jax.random.split(key, 11)
    f32 = jnp.float32
    def nrm(k, shape, scale):
        return jax.random.normal(k, shape, f32) * scale
    return {
        'x': nrm(ks[0], (BATCH, SEQ, D_MODEL), 1.0),
        'norm1_w': 1.0 + nrm(ks[1], (DEPTH, D_MODEL), 0.02),
        'w_in': nrm(ks[2], (DEPTH, D_MODEL, IN_COLS), D_MODEL ** -0.5),
        'w_conv': nrm(ks[3], (DEPTH, CONV_K, CONV_WIDTH), CONV_K ** -0.5),
        'ret_gn_w': 1.0 + nrm(ks[4], (DEPTH, RET_WIDTH), 0.02),
        'w_o': nrm(ks[5], (DEPTH, MIX_WIDTH, D_MODEL), MIX_WIDTH ** -0.5),
        'norm2_w': 1.0 + nrm(ks[6], (DEPTH, D_MODEL), 0.02),
        'w_gate': nrm(ks[7], (DEPTH, D_MODEL, D_FF), D_MODEL ** -0.5),
        'w_up': nrm(ks[8], (DEPTH, D_MODEL, D_FF), D_MODEL ** -0.5),
        'w_down': nrm(ks[9], (DEPTH, D_FF, D_MODEL), D_FF ** -0.5),
        'final_norm_w': 1.0 + nrm(ks[10], (D_MODEL,), 0.02),
    }


def reference(x, norm1_w, w_in, w_conv, ret_gn_w, w_o, norm2_w, w_gate, w_up, w_down, final_norm_w):
    b, s, _ = x.shape
    dt = x.dtype
    pos = jnp.arange(s, dtype=jnp.float32)
    split_at = [RET_WIDTH, 2 * RET_WIDTH, 3 * RET_WIDTH, 4 * RET_WIDTH,
                4 * RET_WIDTH + CONV_WIDTH, 4 * RET_WIDTH + 2 * CONV_WIDTH]
    for layer in range(DEPTH):
        h = rmsnorm(x, norm1_w[layer])
        proj = h @ w_in[layer]
        q, k, v, g, cb, cc, ch = jnp.split(proj, split_at, axis=-1)

        qh = q.astype(jnp.float32).reshape(b, s, RET_HEADS, RET_HEAD_DIM)
        kh = k.astype(jnp.float32).reshape(b, s, RET_HEADS, RET_HEAD_DIM)
        vh = v.astype(jnp.float32).reshape(b, s, RET_HEADS, RET_HEAD_DIM)
        qh = rotary(qh, pos) * (RET_HEAD_DIM ** -0.5)
        kh = rotary(kh, pos)
        r = chunk_retention(qh, kh, vh)
        r = r * lax.rsqrt(jnp.mean(r * r, axis=-1, keepdims=True) + NORM_EPS)
        r = r.reshape(b, s, RET_WIDTH) * ret_gn_w[layer].astype(jnp.float32)
        ret_out = (r * jax.nn.silu(g.astype(jnp.float32))).astype(dt)

        conv_out = cb * causal_depthwise_conv(cc * ch, w_conv[layer])

        mix = jnp.concatenate([ret_out, conv_out], axis=-1)
        x = x + mix @ w_o[layer]

        h2 = rmsnorm(x, norm2_w[layer])
        x = x + (jax.nn.silu(h2 @ w_gate[layer]) * (h2 @ w_up[layer])) @ w_down[layer]
    return rmsnorm(x, final_norm_w)
```

```python
import numpy as np
import concourse.bass as bass
import concourse.mybir as mybir
from concourse.bass_utils import run_bass_kernel_spmd

F32 = mybir.dt.float32
BF16 = mybir.dt.bfloat16
AF = mybir.ActivationFunctionType
ALU = mybir.AluOpType
_DTSZ = {F32: 4, BF16: 2}

D = 1024
SEQ = 4096
NB = 4
TOK = 2048
NT = TOK // 128
NG = TOK // 512
DFF = 2816
EPS = 1e-6
NCORES = 8


def _region(ap):
    t = ap.tensor
    cls = type(t).__name__
    if cls.startswith("DRam"):
        return None
    esz = _DTSZ.get(ap.dtype, 4)
    dims = [list(d) for d in ap.ap]
    pstep, pcnt = dims[0]
    off = ap.offset
    if pstep > 0:
        p0 = off // pstep
        foff = off % pstep
    else:
        p0 = 0
        foff = off
    lo = foff
    hi = foff
    for st, cnt in dims[1:]:
        ext = st * (cnt - 1)
        if ext >= 0:
            hi += ext
        else:
            lo += ext
    b0 = lo * esz
    b1 = (hi + 1) * esz
    if cls.startswith("PSum"):
        b0 = (b0 // 2048) * 2048
        b1 = ((b1 + 2047) // 2048) * 2048
        return (ap.name, 0, 128, b0, b1, True)
    return (ap.name, p0, p0 + pcnt, b0, b1, False)


class Op:
    __slots__ = ("eng", "fn", "reads", "writes", "dma", "slot", "idx", "waits", "sig", "deps", "need", "label",
                 "cost", "alldeps", "fin", "nbytes")

    def __init__(self, eng, fn, reads, writes, dma, slot):
        self.eng = eng
        self.fn = fn
        self.reads = reads
        self.writes = writes
        self.dma = dma
        self.slot = slot
        self.waits = []
        self.sig = None
        self.deps = set()
        self.alldeps = set()
        self.need = False
        self.fin = 0.0
        self.nbytes = 0


class Sched:
    ENGS = ("pe", "act", "dve", "pool", "sp")
    WINDOW = {"pe": 16, "act": 6, "dve": 6, "pool": 6, "sp": 4}
    RAW_ONLY = False

    def __init__(self, nc):
        self.nc = nc
        self.ops = []
        self.label = ""
        self.reorder = True

    def add(self, eng, fn, reads=(), writes=(), dma=False, slot=None):
        op = Op(eng, fn, [r for r in (_region(a) for a in reads) if r],
                [r for r in (_region(a) for a in writes) if r], dma, slot)
        op.idx = len(self.ops)
        op.label = self.label
        n = 512
        aps = list(writes) + list(reads)
        if aps:
            n = aps[0].free_size()
        if dma:
            op.nbytes = max(a.free_size() * a.partition_size() * _DTSZ.get(a.dtype, 4) for a in aps)
            op.cost = 1200.0 if eng == "pool" else 100.0
        elif eng == "act":
            op.cost = (224 + n) / 1.2
        elif eng == "dve":
            op.cost = (64 + n) / 0.96
        elif eng == "pool":
            op.cost = 150 + 2.2 * n
        else:
            op.cost = 100.0
        self.ops.append(op)
        return op

    def _analyze(self):
        hist = {}
        ops = self.ops
        for op in ops:
            accs = [(r, False) for r in op.reads] + [(r, True) for r in op.writes]
            for (name, p0, p1, b0, b1, is_psum), w in accs:
                lst = hist.setdefault(name, [])
                for h in lst:
                    if h[5] == op.idx:
                        continue
                    if h[0] < p1 and p0 < h[1] and h[2] < b1 and b0 < h[3]:
                        if w or h[4] or (is_psum and h[6] != op.eng):
                            op.deps.add((h[5], h[4], w))
            for (name, p0, p1, b0, b1, is_psum), w in accs:
                lst = hist[name]
                if w:
                    lst[:] = [h for h in lst if not (p0 <= h[0] and h[1] <= p1 and b0 <= h[2] and h[3] <= b1)]
                lst.append([p0, p1, b0, b1, w, op.idx, op.eng, op.dma])
        for op in ops:
            real = set()
            for (j, jw, iw) in op.deps:
                d = ops[j]
                op.alldeps.add(j)
                if (not d.dma) and (not op.dma) and d.eng == op.eng:
                    if op.eng == "pe":
                        continue
                    if self.RAW_ONLY and not (jw and not iw):
                        continue
                real.add(j)
            op.deps = real

    def _schedule(self):
        ops = self.ops
        pend = {e: [o for o in ops if o.eng == e] for e in self.ENGS}
        if not self.reorder:
            return pend
        out = {e: [] for e in self.ENGS}
        free = {e: 0.0 for e in self.ENGS}
        done = [False] * len(ops)
        dma_free = [0.0]
        LAT = 300.0
        remaining = len(ops)
        while remaining:
            best = None
            for e in self.ENGS:
                lst = pend[e]
                if not lst:
                    continue
                W = self.WINDOW[e]
                cand = None
                for k in range(min(W, len(lst))):
                    o = lst[k]
                    ok = True
                    rdy = 0.0
                    for j in o.alldeps:
                        if not done[j]:
                            ok = False
                            break
                        d = ops[j]
                        t = d.fin + (0.0 if (d.eng == e and not d.dma) else LAT)
                        if t > rdy:
                            rdy = t
                    if not ok:
                        continue
                    st = max(free[e], rdy)
                    key = (st + 40.0 * k, o.idx)
                    if cand is None or key < cand[0]:
                        cand = (key, st, k, o)
                    if st <= free[e]:
                        break
                if cand is not None and (best is None or cand[0] < best[0]):
                    best = cand + (e,)
            key, st, k, o, e = best
            pend[e].pop(k)
            out[e].append(o)
            if o.dma:
                free[e] = st + o.cost
                t0 = max(st + o.cost, dma_free[0])
                dma_free[0] = t0 + o.nbytes / 250.0
                o.fin = dma_free[0] + 2000.0
            else:
                free[e] = st + o.cost
                o.fin = free[e]
            done[o.idx] = True
            remaining -= 1
        self.makespan = max(o.fin for o in ops)
        return out

    def build(self, final_wait_eng="sp"):
        nc = self.nc
        self._analyze()
        streams = self._schedule()
        ops = self.ops
        pos = {}
        for e in self.ENGS:
            for i, op in enumerate(streams[e]):
                pos[op.idx] = i
        for op in ops:
            last = {}
            keep = set()
            for j in op.deps:
                d = ops[j]
                if d.dma:
                    keep.add(j)
                elif d.eng not in last or pos[j] > pos[last[d.eng]]:
                    last[d.eng] = j
            keep.update(last.values())
            op.deps = keep
            for j in keep:
                ops[j].need = True
        sems = {}
        cnt = {}
        for op in ops:
            if op.dma:
                key = "d_" + op.slot
                cnt[key] = cnt.get(key, 0) + 16
                op.sig = (key, cnt[key], 16)
        for e in self.ENGS:
            for op in streams[e]:
                if (not op.dma) and op.need:
                    key = "e_" + op.eng
                    cnt[key] = cnt.get(key, 0) + 1
                    op.sig = (key, cnt[key], 1)
        for e in self.ENGS:
            sd = {}
            for op in streams[e]:
                w = {}
                for j in op.deps:
                    k, v, _ = ops[j].sig
                    if v > w.get(k, 0):
                        w[k] = v
                for k, v in w.items():
                    if sd.get(k, 0) < v:
                        sd[k] = v
                        op.waits.append((k, v))
        final = [(k, v) for k, v in cnt.items() if k.startswith("d_")]
        for k in cnt:
            sems[k] = nc.alloc_semaphore("s_" + k)

        def emit(eng_name, eng):
            for op in streams[eng_name]:
                for k, v in op.waits:
                    eng.wait_ge(sems[k], v)
                ins = op.fn(eng)
                if op.sig is not None:
                    ins.then_inc(sems[op.sig[0]], op.sig[2])
            if eng_name == final_wait_eng:
                for k, v in final:
                    eng.wait_ge(sems[k], v)

        with nc.Block() as block:
            @block.tensor
            def _(e):
                emit("pe", e)

            @block.scalar
            def _(e):
                emit("act", e)

            @block.vector
            def _(e):
                emit("dve", e)

            @block.gpsimd
            def _(e):
                emit("pool", e)

            @block.sync
            def _(e):
                emit("sp", e)
        self.streams = streams
        return {e: len(s) for e, s in streams.items()}, len(sems)


def build_program(debug=False):
    nc = bass.Bass("TRN2", target_bir_lowering=False)
    S = Sched(nc)
    dbg_n = [0]

    def dump(name, ap, shape, dt):
        if not debug:
            return
        o = nc.dram_tensor("dbg_" + name, list(shape), dt, kind="ExternalOutput").ap()
        dbg_n[0] += 1
        S.add("sp", lambda e: e.dma_start(out=o, in_=ap), reads=[ap], dma=True, slot="dbg%d" % dbg_n[0])

    def din(name, shape):
        return nc.dram_tensor(name, list(shape), F32, kind="ExternalInput").ap()

    xm = din("xm", [TOK, D])
    xp = din("xp", [TOK, D])
    w_in = din("w_in", [D, 3584])
    w_o = din("w_o", [D, D])
    w_gate = din("w_gate", [D, DFF])
    w_up = din("w_up", [D, DFF])
    w_down = din("w_down", [DFF, D])
    nrm = din("nrm", [3, D])
    gnw = din("gnw", [128, 4])
    wcv = din("wcv", [128, 12])
    rot_m = din("rot_m", [2, 128, TOK])
    rot_p = din("rot_p", [2, 128, TOK])
    pww = din("pww", [128, NT * 4])
    cmat = din("cmat", [3, 128, 128])
    cmask = din("cmask", [128, 512])
    cxi = din("cxi", [128, 512])
    czeta = din("czeta", [128, 4])
    y = nc.dram_tensor("y", [TOK, D], F32, kind="ExternalOutput").ap()

    A = nc.alloc_sbuf_tensor
    x1 = A("x1", [128, NT, D], F32)
    x1b = x1[:, :, :].bitcast(BF16).rearrange("p t n -> p (t n)")
    WR = A("WR", [128, 40960], BF16)
    FR = A("FR", [128, 6664], F32)
    xs1 = A("xs1", [128, D], BF16)
    rt = A("rt", [128, 2, 2, 512], F32)
    gv = A("gv", [128, 2, D], F32)
    maskp = A("maskp", [128, 512], F32)
    xis = A("xis", [128, 512], F32)
    zeta = A("zeta", [128, 4], F32)
    pwt = A("pwt", [128, NT * 4], F32)
    gn = A("gn", [128, 4], F32)
    wc = A("wc", [128, 12], F32)
    nh = A("nh", [128, 512], F32)
    cm = A("cm", [128, 3, 128], BF16)
    stat = A("stat", [128, 3, 16], F32)
    st4 = A("st4", [128, 2, 16], F32)
    st4b = A("st4b", [128, 2, 16], BF16)
    cmf = A("cmf", [128, 2, 128], F32)
    Sst = A("Sst", [128, 512], F32)
    Sb = A("Sb", [128, 3, 512], BF16)
    sTm = A("sTm", [128, 2, 512], BF16)
    r2 = A("r2", [128, 512], BF16)
    junk = A("junk", [128, D], BF16)
    uh = A("uh", [128, 4, 2], F32)
    ps = [nc.alloc_psum_tensor("ps%d" % i, [128, 512], F32) for i in range(8)]
    ident = cm[:, 0, :]
    pswap = cm[:, 1, :]
    ones = cm[:, 2, :]

    def wr(off, n):
        return WR[:, off:off + n]
    ring = [wr(i * 4096, 4096).rearrange("p (k n) -> p k n", k=8) for i in range(3)]
    Wo = wr(12288, 8192).rearrange("p (k n) -> p k n", k=8)
    hT = wr(20480, 4096).rearrange("p (k n) -> p k n", k=8)
    qT = wr(24576, 2048).rearrange("p (h n) -> p h n", h=4)
    kT = wr(26624, 2048).rearrange("p (h n) -> p h n", h=4)
    kz = wr(28672, 2048).rearrange("p (t n) -> p t n", t=4)
    vv = wr(30720, 2048).rearrange("p (t n) -> p t n", t=4)
    sg = wr(32768, 2048).rearrange("p (h n) -> p h n", h=4)
    mixT = wr(34816, 4096).rearrange("p (k n) -> p k n", k=8)
    xs = [wr(38912, 1024)]
    qb = [wr(39936, 512), wr(40448, 512)]
    h2T = [wr(o, 4096).rearrange("p (k n) -> p k n", k=8) for o in (20480, 24576, 12288, 16384)]
    fring = [(wr(o, 4096).rearrange("p (k n) -> p k n", k=8),
              wr(o + 4096, 4096).rearrange("p (k n) -> p k n", k=8),
              wr(o + 8192, 4096).rearrange("p (c n) -> p c n", c=4)) for o in (0, 28672)]
    def xb(off, n):
        return x1b[:, off:off + n]
    Wk0 = wr(28672, 4096).rearrange("p (k n) -> p k n", k=8)
    Wv0 = wr(32768, 4096).rearrange("p (k n) -> p k n", k=8)
    Wcc0 = xb(8192, 4096).rearrange("p (k n) -> p k n", k=8)
    Wch0 = xb(12288, 4096).rearrange("p (k n) -> p k n", k=8)
    kw = xb(16384, 8192).rearrange("p (t n) -> p t n", t=16)
    vpre = xb(24576, 8192).rearrange("p (t n) -> p t n", t=16)

    def fr(off, n):
        return FR[:, off:off + n]
    ra = [fr(0, 512), fr(512, 512)]
    rb = [fr(1024, 512), fr(1536, 512)]
    ccs = [fr(2048, 512), fr(2560, 512)]
    ub = [fr(3072, 514), fr(3586, 514)]
    yb = [fr(4100, 512), fr(4612, 512)]
    msq = fr(5124, 512)
    retb = [fr(5636, 512), fr(6148, 512)]
    stg = [fr(2048, 1024), fr(3072, 1024),
           WR[:, 24576:26624].bitcast(F32), WR[:, 36864:38912].bitcast(F32)]
    sgt = [fr(0, 512), fr(512, 512)]
    ost = [fr(1024, 1024), fr(2048, 1024), fr(5632, 1024)]
    xs2 = FR[:, 5120:5632].bitcast(BF16)
    XS01 = [xs[0], xs1[:, :]]
    XS2 = [xs2, xs1[:, :]]
    actT = [FR[:, 3072:4096].bitcast(BF16).rearrange("p (c n) -> p c n", c=4),
            FR[:, 4096:5120].bitcast(BF16).rearrange("p (c n) -> p c n", c=4)]

    CDEC = [float(np.float32(np.exp(128.0 * np.log1p(-np.exp2(-5.0 - h))))) for h in range(4)]
    bank_i = [0]

    def nb():
        b = ps[bank_i[0] % 8]
        bank_i[0] += 1
        return b

    def dma(q, out, in_, slot):
        S.add(q, lambda e: e.dma_start(out=out, in_=in_), reads=[in_], writes=[out], dma=True, slot=slot)

    def mm(out, lhsT, rhs, start, stop):
        op = S.add("pe", lambda e: e.matmul(out, lhsT, rhs, start=start, stop=stop), reads=[lhsT, rhs], writes=[out])
        op.cost = max(100.0, rhs.free_size() / 2.33) * (4.0 if rhs.dtype == F32 else 1.0)

    def tr(out, in_):
        op = S.add("pe", lambda e: e.transpose(out, in_, ident), reads=[in_, ident], writes=[out])
        op.cost = 100.0

    def act(out, in_, func, scale=1.0, bias=0.0, accum=None):
        if accum is None:
            S.add("act", lambda e: e.activation(out=out, in_=in_, func=func, scale=scale, bias=bias),
                  reads=[in_], writes=[out])
        else:
            S.add("act", lambda e: e.activation(out=out, in_=in_, func=func, accum_out=accum),
                  reads=[in_], writes=[out, accum])

    def acopy(out, in_):
        S.add("act", lambda e: e.copy(out, in_), reads=[in_], writes=[out])

    def tt(eng, out, a, b, op):
        assert out.free_size() == a.free_size() == b.free_size(), (out.shape, a.shape, b.shape, repr(out), repr(a), repr(b))
        S.add(eng, lambda e: e.tensor_tensor(out, a, b, op), reads=[a, b], writes=[out])

    def tsc(eng, out, a, s1, s2, op0, op1):
        rd = [a] + [s for s in (s1, s2) if not isinstance(s, (int, float)) and s is not None]
        S.add(eng, lambda e: e.tensor_scalar(out, a, s1, s2, op0, op1), reads=rd, writes=[out])

    def stt(eng, out, in0, sc, in1, op0, op1):
        rd = [in0, in1] + ([] if isinstance(sc, (int, float)) else [sc])
        S.add(eng, lambda e: e.scalar_tensor_tensor(out, in0, sc, in1, op0, op1), reads=rd, writes=[out])

    def cp(eng, out, in_):
        S.add(eng, lambda e: e.tensor_copy(out, in_), reads=[in_], writes=[out])

    dma("pool", cm[:, :, :], cmat.rearrange("c p n -> p c n"), "cm")
    S.add("pool", lambda e: e.memset(nh[:, :], -0.5), writes=[nh[:, :]])

    def late_consts():
        dma("sp", pwt[:, :], pww[:, :], "c4")
        dma("sp", zeta[:, :], czeta[:, :], "c3")
        dma("sp", maskp[:, :], cmask[:, :], "c0")
        dma("sp", xis[:, :], cxi[:, :], "c1")
        dma("sp", gn[:, :], gnw[:, :], "c5")
        dma("sp", wc[:, :], wcv[:, :], "c6")
        dma("sp", gv[:, 1, :], nrm[1:2, :].partition_broadcast(128), "gv1")
        dma("sp", cmf[:, 0, :], cmat[0, :, :], "cmf0")
        dma("sp", cmf[:, 1, :], cmat[2, :, :], "cmf1")
        S.add("pool", lambda e: e.memset(uh[:, :, :], 0.0), writes=[uh[:, :, :]])

    def wblock(dst, col0, slot):
        dma("pool", dst, w_in[:, col0:col0 + 512].rearrange("(k p) n -> p k n", p=128), slot)

    wblock(Wk0, 512, "wk0")
    wblock(Wv0, 1024, "wv0")

    stat_i = [0]

    def norm_A(src, gidx, xsb):
        S.label = 'normA'
        c = stat_i[0] % 16
        stat_i[0] += 1
        ss = stat[:, 0, c:c + 1]
        ms = stat[:, 1, c:c + 1]
        rs = stat[:, 2, c:c + 1]
        act(junk[:, :], src, AF.Square, accum=ss)
        tsc("pool", ms, ss, 1.0 / D, EPS, ALU.mult, ALU.add)
        tt("pool", rs, ms, nh[:, 0:1], ALU.pow)
        stt("dve", xsb, src, rs, gv[:, gidx, :], ALU.mult, ALU.mult)

    def norm_B(xsb, dst):
        S.label = 'normB'
        b = nb()
        bb = b[:, :].bitcast(BF16)
        for kc in range(8):
            tr(bb[:, kc * 128:(kc + 1) * 128], xsb[:, kc * 128:(kc + 1) * 128])
        acopy(dst, bb.rearrange("p (k n) -> p k n", k=8))

    def norm_pipe(tiles, slots, fillers=(), pre=None, skipA=0):
        fl = list(fillers)
        n = len(tiles)
        for i in range(skipA, min(2, n)):
            if pre is not None:
                pre(i)
            norm_A(tiles[i][0], tiles[i][1], slots[i % 2])
        for i in range(n):
            if fl:
                fl.pop(0)()
            norm_B(slots[i % 2], tiles[i][2])
            if i + 2 < n:
                if pre is not None:
                    pre(i + 2)
                norm_A(tiles[i + 2][0], tiles[i + 2][1], slots[i % 2])
        while fl:
            fl.pop(0)()

    rot_i = [0]

    def proj_fm(W, c, rhs_hT, n=512):
        b = nb()
        for kc in range(8):
            mm(b[:, 0:n], W[:, kc, c * 128:(c + 1) * 128], rhs_hT[:, kc, :], kc == 0, kc == 7)
        return b

    def rotary_a(b, rslot):
        i = rot_i[0] % 2
        rot_i[0] += 1
        acopy(qb[i], b[:, :])
        tt("dve", ra[i], b[:, :], rt[:, rslot, 0, :], ALU.mult)
        return i

    def rotary_b(i, rslot, out, xi):
        b2 = nb()
        mm(b2[:, :], pswap, qb[i], True, True)
        tt("dve", rb[i], b2[:, :], rt[:, rslot, 1, :], ALU.mult)
        if xi is None:
            tt("pool", out, ra[i], rb[i], ALU.add)
        else:
            tt("pool", ra[i], ra[i], rb[i], ALU.add)
            tt("pool", out.rearrange("p (c i) -> p c i", i=128), ra[i].rearrange("p (c i) -> p c i", i=128),
               xi.unsqueeze(1).to_broadcast([128, 4, 128]), ALU.mult)

    def proj_rot(W, dstT, rslot, with_xi, hsrc=None):
        S.label = 'projrot'
        hsrc = hT if hsrc is None else hsrc
        pend = None
        for hd in range(4):
            b = proj_fm(W, hd, hsrc)
            st = rotary_a(b, rslot)
            if pend is not None:
                rotary_b(*pend)
            pend = (st, rslot, dstT[:, hd, :], xis[:, hd * 128:(hd + 1) * 128] if with_xi else None)
        rotary_b(*pend)

    def load_rt(tab, g, slot):
        dma("sp", rt[:, slot, :, :], tab[:, :, g * 512:(g + 1) * 512].rearrange("c p n -> p c n"), "rt%d" % slot)

    rt_n = [0]
    blk_cols = [512, 0, 1024, 1536, 2560, 3072, 2048]
    ring_n = [0]
    pending_blocks = []

    def issue_block(col0):
        s = ring_n[0] % 3
        ring_n[0] += 1
        wblock(ring[s], col0, "ring%d" % s)
        return ring[s]

    def load_x_group(g):
        for tl in range(4):
            t = g * 4 + tl
            dma("sp", x1[:, t, :], xm[t * 128:(t + 1) * 128, :], "x%d" % t)

    def x_tiles(g, gidx, dstT):
        return [(x1[:, g * 4 + tl, :], gidx, dstT[:, :, tl * 128:(tl + 1) * 128]) for tl in range(4)]

    def ktr0(pg, tl):
        def f():
            S.label = 'ktr0'
            t = pg * 4 + tl
            b = nb()
            bb = b[:, :].bitcast(BF16)
            for hd in range(4):
                tr(bb[:, hd * 128:(hd + 1) * 128], kT[:, hd, tl * 128:(tl + 1) * 128])
            tt("dve", kw[:, t, :].rearrange("p (h d) -> p h d", h=4),
               bb[:, 0:512].rearrange("p (h d) -> p h d", h=4),
               pwt[:, t * 4:(t + 1) * 4].unsqueeze(2).to_broadcast([128, 4, 128]), ALU.mult)
        return f

    hTp = [hT, wr(12288, 4096).rearrange("p (k n) -> p k n", k=8)]
    for pg in range(NG):
        rslot = rt_n[0] % 2
        rt_n[0] += 1
        hcur = hTp[pg % 2]
        if pg > 0:
            load_rt(rot_p, pg, rslot)
        if pg == 0:
            dma("act", gv[:, 0, :], nrm[0:1, :].partition_broadcast(128), "gv0")
            for tl in range(4):
                dma("sp", x1[:, tl, :], xp[tl * 128:(tl + 1) * 128, :], "x%d" % tl)
            tiles = [(x1[:, tl, :], 0, hcur[:, :, tl * 128:(tl + 1) * 128]) for tl in range(4)]
            norm_pipe(tiles, XS01)
            load_rt(rot_p, 0, 0)
            late_consts()
        else:
            tiles = [(stg[tl], 0, hcur[:, :, tl * 128:(tl + 1) * 128]) for tl in range(4)]
            fill = [ktr0(pg - 1, tl) for tl in range(4)]
            norm_pipe(tiles, XS01, fill, skipA=2)
        if pg + 1 < NG:
            for i in range(4):
                t = (pg + 1) * 4 + i
                dma("sp", stg[i], xp[t * 128:(t + 1) * 128, :], "stg%d" % i)
        proj_rot(Wk0, kT, rslot, False, hcur)
        if pg + 1 < NG:
            for i in range(2):
                norm_A(stg[i], 0, XS01[i])
        S.label = 'v0'
        for tl in range(4):
            t = pg * 4 + tl
            b = nb()
            for kc in range(8):
                mm(b[:, :], hcur[:, kc, tl * 128:(tl + 1) * 128], Wv0[:, kc, :], kc == 0, kc == 7)
            acopy(vpre[:, t, :], b[:, :])
        if pg == 1:
            wblock(Wcc0, 2560, "wcc0")
            wblock(Wch0, 3072, "wch0")
            load_x_group(0)
        if pg == 2:
            blocks = [issue_block(blk_cols[0]), issue_block(blk_cols[1]), issue_block(blk_cols[2])]
    hlast = hTp[(NG - 1) % 2]
    for i in range(2):
        norm_A(x1[:, i, :], 0, XS01[i])
    for tl in range(4):
        ktr0(NG - 1, tl)()
    S.label = 'halo'
    for c in range(4):
        b1 = nb()
        for kc in range(8):
            mm(b1[:, 0:2], Wcc0[:, kc, c * 128:(c + 1) * 128], hlast[:, kc, 510:512], kc == 0, kc == 7)
        b2 = nb()
        for kc in range(8):
            mm(b2[:, 0:2], Wch0[:, kc, c * 128:(c + 1) * 128], hlast[:, kc, 510:512], kc == 0, kc == 7)
        acopy(ccs[0][:, 0:2], b1[:, 0:2])
        tt("dve", uh[:, c, :], b2[:, 0:2], ccs[0][:, 0:2], ALU.mult)
    S.label = 'S0'
    b = nb()
    for hd in range(4):
        for t in range(NT):
            mm(b[:, hd * 128:(hd + 1) * 128], kw[:, t, hd * 128:(hd + 1) * 128],
               vpre[:, t, hd * 128:(hd + 1) * 128], t == 0, t == NT - 1)
    cp("dve", Sst[:, :], b[:, :])
    acopy(Sb[:, 0, :], b[:, :])
    dump("S0", Sst[:, :], [128, 512], F32)
    dump("uh0", uh[:, :, :], [128, 4, 2], F32)

    dma("pool", Wo, w_o.rearrange("(k p) n -> p k n", p=128), "wo")
    norm_pipe(x_tiles(0, 0, hT), XS01, skipA=2)
    fgs = [(0, 384), (384, 512), (896, 512), (1408, 512), (1920, 512), (2432, 384)]
    NFG = len(fgs)

    def load_fg(fi):
        f0, w = fgs[fi]
        s = fi % 2
        Wg_, Wu_, Wd_ = fring[s]
        dma("pool", Wg_[:, :, 0:w], w_gate[:, f0:f0 + w].rearrange("(k p) n -> p k n", p=128), "fg%d" % s)
        dma("pool", Wu_[:, :, 0:w], w_up[:, f0:f0 + w].rearrange("(k p) n -> p k n", p=128), "fu%d" % s)
        dma("pool", Wd_[:, 0:w // 128, :], w_down[f0:f0 + w, :].rearrange("(c p) n -> p c n", p=128), "fd%d" % s)

    chunk_par = [0]

    for g in range(NG):
        rslot = rt_n[0] % 2
        rt_n[0] += 1
        load_rt(rot_m, g, rslot)
        if g + 1 < NG:
            load_x_group(g + 1)
        Wk_, Wq_, Wv_ = blocks[0], blocks[1], blocks[2]
        proj_rot(Wk_, kT, rslot, False)
        Wg_ = issue_block(blk_cols[3])
        proj_rot(Wq_, qT, rslot, True)
        Wcc_ = issue_block(blk_cols[4])
        S.label = 'ktr'
        for tl in range(4):
            b = nb()
            bb = b[:, :].bitcast(BF16)
            for hd in range(4):
                tr(bb[:, hd * 128:(hd + 1) * 128], kT[:, hd, tl * 128:(tl + 1) * 128])
            tt("dve", kz[:, tl, :].rearrange("p (h d) -> p h d", h=4),
               bb[:, 0:512].rearrange("p (h d) -> p h d", h=4),
               zeta[:, :].unsqueeze(2).to_broadcast([128, 4, 128]), ALU.mult)

        S.label = 'v'
        for tl in range(4):
            b = nb()
            for kc in range(8):
                mm(b[:, :], hT[:, kc, tl * 128:(tl + 1) * 128], Wv_[:, kc, :], kc == 0, kc == 7)
            acopy(vv[:, tl, :], b[:, :])
        Wch_ = issue_block(blk_cols[5])
        def ret_front(tl):
            S.label = 'front'
            c0 = tl * 128
            bs = nb()
            for hd in range(4):
                mm(bs[:, hd * 128:(hd + 1) * 128], kT[:, hd, c0:c0 + 128], qT[:, hd, c0:c0 + 128], True, True)
            sl = tl % 2
            tt("dve", sTm[:, sl, :], bs[:, :], maskp[:, :], ALU.mult)
            bi = nb()
            for hd in range(4):
                mm(bi[:, hd * 128:(hd + 1) * 128], kz[:, tl, hd * 128:(hd + 1) * 128],
                   vv[:, tl, hd * 128:(hd + 1) * 128], True, True)
            n = chunk_par[0]
            chunk_par[0] += 1
            for hd in range(4):
                hs = slice(hd * 128, (hd + 1) * 128)
                stt("dve", Sst[:, hs], Sst[:, hs], CDEC[hd], bi[:, hs], ALU.mult, ALU.add)
            acopy(Sb[:, (n + 1) % 3, :], Sst[:, :])

        def ret_back(tl, n0):
            S.label = 'back'
            c0 = tl * 128
            sl = tl % 2
            br = nb()
            for hd in range(4):
                hs = slice(hd * 128, (hd + 1) * 128)
                mm(br[:, hs], vv[:, tl, hs], sTm[:, sl, hs], True, False)
                mm(br[:, hs], Sb[:, n0 % 3, hs], qT[:, hd, c0:c0 + 128], False, True)
            act(r2[:, :], br[:, :], AF.Square)
            ret3 = retb[tl % 2].rearrange("p (h i) -> p h i", h=4)
            tt("dve", ret3, br[:, :].rearrange("p (h i) -> p h i", h=4),
               gn[:, :].unsqueeze(2).to_broadcast([128, 4, 128]), ALU.mult)
            tt("pool", ret3, ret3, sg[:, :, c0:c0 + 128], ALU.mult)

        def ret_nA(tl):
            S.label = 'rnormA'
            bq = nb()
            for hd in range(4):
                mm(bq[:, hd:hd + 1], r2[:, hd * 128:(hd + 1) * 128], ones[:, 0:1], True, True)
            c = stat_i[0] % 4
            stat_i[0] += 1
            ms4 = st4[:, 0, c * 4:(c + 1) * 4]
            rs4 = st4[:, 1, c * 4:(c + 1) * 4]
            tsc("dve", ms4, bq[:, 0:4], 1.0 / 128, EPS, ALU.mult, ALU.add)
            tt("pool", rs4, ms4, nh[:, 0:4], ALU.pow)
            hi = st4b[:, 0, c * 4:(c + 1) * 4]
            lo = st4b[:, 1, c * 4:(c + 1) * 4]
            cp("pool", hi, rs4)
            tt("pool", lo, rs4, hi, ALU.subtract)
            return (hi, lo)

        def ret_nB(tl, hl):
            c0 = tl * 128
            hi, lo = hl
            bq2 = nb()
            S.label = 'rnormH'
            for hd in range(4):
                mm(bq2[:, hd * 128:(hd + 1) * 128], hi[:, hd:hd + 1].to_broadcast([128, 128]), ident, True, False)
                mm(bq2[:, hd * 128:(hd + 1) * 128], lo[:, hd:hd + 1].to_broadcast([128, 128]), ident, False, True)
            S.label = 'rnormB'
            tt("dve", mixT[:, 0:4, c0:c0 + 128], retb[tl % 2].rearrange("p (h i) -> p h i", h=4),
               bq2[:, :].rearrange("p (h i) -> p h i", h=4), ALU.mult)

        def g_chunk(c):
            S.label = 'g'
            b = proj_fm(Wg_, c, hT)
            act(sg[:, c, :], b[:, :], AF.Silu)

        conv_i = [0]

        def conv_a(c):
            S.label = 'conv'
            i = c % 2
            b1 = proj_fm(Wcc_, c, hT)
            acopy(ccs[i], b1[:, :])
            b2 = proj_fm(Wch_, c, hT)
            u = ub[i]
            cp("pool", u[:, 0:2], uh[:, c, :])
            tt("dve", u[:, 2:514], b2[:, :], ccs[i], ALU.mult)
            cp("pool", uh[:, c, :], u[:, 512:514])
            yy = yb[i]
            w0 = wc[:, c:c + 1]
            S.add("act", lambda e: e.activation(out=yy, in_=u[:, 0:512], func=AF.Copy, scale=w0),
                  reads=[u[:, 0:512], w0], writes=[yy])
            stt("dve", yy, u[:, 1:513], wc[:, 4 + c:5 + c], yy, ALU.mult, ALU.add)
            stt("dve", yy, u[:, 2:514], wc[:, 8 + c:9 + c], yy, ALU.mult, ALU.add)

        def conv_b(c, Wcb_):
            S.label = 'conv'
            b3 = proj_fm(Wcb_, c, hT)
            tt("dve", mixT[:, 4 + c, :], b3[:, :], yb[c % 2], ALU.mult)

        n_base = chunk_par[0]
        if g + 1 < NG:
            nxt_tiles, nxt_slots = x_tiles(g + 1, 0, hT), XS01
        else:
            nxt_tiles, nxt_slots = x_tiles(0, 1, h2T[0]), XS01
        ret_front(0)
        g_chunk(0)
        g_chunk(1)
        g_chunk(2)
        g_chunk(3)
        Wcb_ = issue_block(blk_cols[6])
        ret_back(0, n_base)
        ret_front(1)
        conv_a(0)
        h0 = ret_nA(0)
        ret_back(1, n_base + 1)
        ret_front(2)
        conv_a(1)
        ret_nB(0, h0)
        h1 = ret_nA(1)
        conv_b(0, Wcb_)
        ret_back(2, n_base + 2)
        ret_front(3)
        conv_a(2)
        ret_nB(1, h1)
        h2_ = ret_nA(2)
        conv_b(1, Wcb_)
        ret_back(3, n_base + 3)
        norm_A(nxt_tiles[0][0], nxt_tiles[0][1], nxt_slots[0])
        conv_a(3)
        ret_nB(2, h2_)
        h3 = ret_nA(3)
        conv_b(2, Wcb_)
        norm_A(nxt_tiles[1][0], nxt_tiles[1][1], nxt_slots[1])
        conv_b(3, Wcb_)
        if g == 0:
            dump("hT", WR[:, 20480:24576], [128, 4096], BF16)
            dump("qT", WR[:, 24576:26624], [128, 2048], BF16)
            dump("kT", WR[:, 26624:28672], [128, 2048], BF16)
            dump("kz", WR[:, 28672:30720], [128, 2048], BF16)
            dump("vv", WR[:, 30720:32768], [128, 2048], BF16)
            dump("sg", WR[:, 32768:34816], [128, 2048], BF16)
            dump("mixT", WR[:, 34816:38912], [128, 4096], BF16)
            dump("Sst", Sst[:, :], [128, 512], F32)
        def wo_tile(tl, g=g):
            def f():
                S.label = 'wo'
                t = g * 4 + tl
                for cb in range(2):
                    b = nb()
                    for kc in range(8):
                        mm(b[:, :], mixT[:, kc, tl * 128:(tl + 1) * 128], Wo[:, kc, cb * 512:(cb + 1) * 512], kc == 0, kc == 7)
                    tt("dve", x1[:, t, cb * 512:(cb + 1) * 512], x1[:, t, cb * 512:(cb + 1) * 512], b[:, :], ALU.add)
            return f
        fill = [wo_tile(tl) for tl in range(4)]
        f0 = fill[0]
        fill[0] = lambda f0=f0, h3=h3: (f0(), ret_nB(3, h3))
        if g + 1 < NG:
            blocks = [issue_block(blk_cols[0]), issue_block(blk_cols[1]), issue_block(blk_cols[2])]
            norm_pipe(nxt_tiles, nxt_slots, fill, skipA=2)
        else:
            load_fg(0)
            norm_pipe(nxt_tiles, nxt_slots, fill, skipA=2)

    dump("x1", x1[:, :, :], [128, NT, D], F32)
    steps = [(fi, tg) for fi in range(NFG) for tg in range(NG)]
    gu_i = [0]

    def gate_up(fi, tg, ai):
        f0, w = fgs[fi]
        Wg_, Wu_, Wd_ = fring[fi % 2]
        rhs = h2T[tg]

        def chunk(fc):
            def f():
                S.label = 'gu'
                bg = proj_fm(Wg_, fc, rhs)
                bu = proj_fm(Wu_, fc, rhs)
                i = gu_i[0] % 2
                gu_i[0] += 1
                act(sgt[i], bg[:, :], AF.Silu)
                tt("dve", actT[ai][:, fc, :], bu[:, :], sgt[i], ALU.mult)
            return f
        fill = [chunk(fc) for fc in range(w // 128)]
        if fi == 0 and tg + 1 < NG:
            norm_pipe(x_tiles(tg + 1, 1, h2T[tg + 1]), XS2, fill)
        else:
            for f in fill:
                f()

    fin_i = [0]

    def down(fi, tg, ai):
        S.label = 'down'
        f0, w = fgs[fi]
        Wg_, Wu_, Wd_ = fring[fi % 2]
        nfc = w // 128
        for tl in range(4):
            t = tg * 4 + tl
            for cb in range(2):
                b = nb()
                for fc in range(nfc):
                    mm(b[:, :], actT[ai][:, fc, tl * 128:(tl + 1) * 128], Wd_[:, fc, cb * 512:(cb + 1) * 512],
                       fc == 0, fc == nfc - 1)
                tt("dve", x1[:, t, cb * 512:(cb + 1) * 512], x1[:, t, cb * 512:(cb + 1) * 512], b[:, :], ALU.add)
            if fi == NFG - 1:
                c = stat_i[0] % 16
                stat_i[0] += 1
                ss = stat[:, 0, c:c + 1]
                ms = stat[:, 1, c:c + 1]
                rs = stat[:, 2, c:c + 1]
                oi = fin_i[0] % 3
                o = ost[oi]
                fin_i[0] += 1
                act(junk[:, :], x1[:, t, :], AF.Square, accum=ss)
                tsc("pool", ms, ss, 1.0 / D, EPS, ALU.mult, ALU.add)
                tt("pool", rs, ms, nh[:, 0:1], ALU.pow)
                xin = x1[:, t, :]
                if t % 2 == 0 or tg == NG - 1:
                    stt("dve", o, xin, rs, gv[:, 0, :], ALU.mult, ALU.mult)
                else:
                    S.add("act", lambda e, o=o, xin=xin, rs=rs: e.activation(out=o, in_=xin, func=AF.Copy, scale=rs),
                          reads=[xin, rs], writes=[o])
                    tt("pool", o, o, gv[:, 0, :], ALU.mult)
                dma("sp", y[t * 128:(t + 1) * 128, :], o, "out%d" % oi)

    dma("sp", gv[:, 0, :], nrm[2:3, :].partition_broadcast(128), "gv0")
    load_fg(1)
    loaded = 2
    for si, (fi, tg) in enumerate(steps):
        gate_up(fi, tg, si % 2)
        if si > 0:
            pfi, ptg = steps[si - 1]
            down(pfi, ptg, (si - 1) % 2)
            if ptg == NG - 1 and loaded < NFG:
                load_fg(loaded)
                loaded += 1
    pfi, ptg = steps[-1]
    down(pfi, ptg, (len(steps) - 1) % 2)

    info = S.build()
    global _LAST_SCHED
    _LAST_SCHED = S
    return nc, info


def _tables():
    h = np.arange(4, dtype=np.float64)
    log_g = np.log1p(-np.exp2(-5.0 - h))
    i128 = np.arange(128)
    jj = i128[:, None]
    ii = i128[None, :]
    same = (jj // 64) == (ii // 64)
    lower = (jj < 64) & (ii >= 64)
    mask = np.zeros((128, 4, 128), np.float64)
    for hd in range(4):
        e_same = np.abs(ii - jj) - (ii + 1.0)
        e_low = (ii - jj) - (ii + 1.0)
        mask[:, hd, :] = np.where(same, np.exp(log_g[hd] * e_same), np.where(lower, np.exp(log_g[hd] * e_low), 0.0))
    xi = np.exp(log_g[:, None] * (i128[None, :] + 1.0)) * (128.0 ** -0.5)
    xi = np.broadcast_to(xi.reshape(1, 512), (128, 512))
    zeta = np.exp(log_g[None, :] * (127.0 - i128[:, None]))
    ident = np.eye(128)
    pswap = np.zeros((128, 128))
    for m in range(128):
        pswap[(m + 64) % 128, m] = 1.0
    cmat = np.stack([ident, pswap, np.ones((128, 128))])
    f32 = np.float32
    return dict(cmask=mask.reshape(128, 512).astype(f32), cxi=np.ascontiguousarray(xi).astype(f32),
                czeta=zeta.astype(f32), cmat=cmat.astype(f32)), log_g


def _rot_table(pos0):
    d = 128
    expo = (np.arange(0, d, 2, dtype=np.float32) / np.float32(d)).astype(np.float64)
    inv_freq = (np.float32(1.0) / (10000.0 ** expo).astype(np.float32)).astype(np.float32)
    pos = np.arange(pos0, pos0 + TOK, dtype=np.float32)
    ang = (pos[None, :] * inv_freq[:, None]).astype(np.float32)
    c = np.cos(ang.astype(np.float64))
    s = np.sin(ang.astype(np.float64))
    cosT = np.concatenate([c, c], axis=0)
    sinT = np.concatenate([-s, s], axis=0)
    return np.stack([cosT, sinT]).astype(np.float32)


_CACHE = {}
_LAST_SCHED = None


def kernel(x, norm1_w, w_in, w_conv, ret_gn_w, w_o, norm2_w, w_gate, w_up, w_down, final_norm_w):
    return _run(x, norm1_w, w_in, w_conv, ret_gn_w, w_o, norm2_w, w_gate, w_up, w_down, final_norm_w)[0]


def _run(x, norm1_w, w_in, w_conv, ret_gn_w, w_o, norm2_w, w_gate, w_up, w_down, final_norm_w, debug=False):
    f32 = np.float32
    x = np.asarray(x, f32)
    key = "nc_dbg" if debug else "nc"
    if key not in _CACHE:
        _CACHE[key] = build_program(debug)
    nc, info = _CACHE[key]
    consts, log_g = _tables()
    nrm = np.stack([np.asarray(norm1_w, f32).reshape(D), np.asarray(norm2_w, f32).reshape(D),
                    np.asarray(final_norm_w, f32).reshape(D)])
    gnw = np.ascontiguousarray(np.asarray(ret_gn_w, f32).reshape(4, 128).T)
    wcv = np.ascontiguousarray(np.asarray(w_conv, f32).reshape(3, 4, 128).transpose(2, 0, 1).reshape(128, 12))
    shared = dict(
        w_in=np.ascontiguousarray(np.asarray(w_in, f32).reshape(D, 3584)),
        w_o=np.ascontiguousarray(np.asarray(w_o, f32).reshape(D, D)),
        w_gate=np.ascontiguousarray(np.asarray(w_gate, f32).reshape(D, DFF)),
        w_up=np.ascontiguousarray(np.asarray(w_up, f32).reshape(D, DFF)),
        w_down=np.ascontiguousarray(np.asarray(w_down, f32).reshape(DFF, D)),
        nrm=nrm, gnw=gnw, wcv=wcv, **consts)
    rots = [_rot_table(0), _rot_table(TOK)]
    j = np.arange(TOK, dtype=np.float64)
    pw1 = np.exp(log_g[None, :] * (TOK - 1.0 - j[:, None]))
    pw1 = pw1.reshape(NT, 128, 4).transpose(1, 0, 2).reshape(128, NT * 4).astype(f32)
    pw0 = np.zeros_like(pw1)
    zeros_x = np.zeros((TOK, D), f32)
    in_maps = []
    for c in range(NCORES):
        b, half = divmod(c, 2)
        m = dict(shared)
        m["xm"] = np.ascontiguousarray(x[b, half * TOK:(half + 1) * TOK])
        m["xp"] = np.ascontiguousarray(x[b, 0:TOK]) if half == 1 else zeros_x
        m["rot_m"] = rots[half]
        m["rot_p"] = rots[0]
        m["pww"] = pw1 if half == 1 else pw0
        in_maps.append(m)
    res = run_bass_kernel_spmd(nc, in_maps, core_ids=list(range(NCORES)))
    out = np.empty((NB, SEQ, D), f32)
    for c in range(NCORES):
        b, half = divmod(c, 2)
        out[b, half * TOK:(half + 1) * TOK] = res.results[c]["y"]
    return out, res
```

```python
import numpy as np
import concourse.bass as bass
import concourse.mybir as mybir
from concourse.bass_utils import run_bass_kernel_spmd

F32 = mybir.dt.float32
BF16 = mybir.dt.bfloat16
AF = mybir.ActivationFunctionType
ALU = mybir.AluOpType
_DTSZ = {F32: 4, BF16: 2}

D = 1024
SEQ = 4096
NB = 4
TOK = 2048
NT = TOK // 128
NG = TOK // 512
DFF = 2816
EPS = 1e-6
NCORES = 8


def _region(ap):
    t = ap.tensor
    cls = type(t).__name__
    if cls.startswith("DRam"):
        return None
    esz = _DTSZ.get(ap.dtype, 4)
    dims = [list(d) for d in ap.ap]
    pstep, pcnt = dims[0]
    off = ap.offset
    if pstep > 0:
        p0 = off // pstep
        foff = off % pstep
    else:
        p0 = 0
        foff = off
    lo = foff
    hi = foff
    for st, cnt in dims[1:]:
        ext = st * (cnt - 1)
        if ext >= 0:
            hi += ext
        else:
            lo += ext
    b0 = lo * esz
    b1 = (hi + 1) * esz
    if cls.startswith("PSum"):
        b0 = (b0 // 2048) * 2048
        b1 = ((b1 + 2047) // 2048) * 2048
        return (ap.name, 0, 128, b0, b1, True)
    return (ap.name, p0, p0 + pcnt, b0, b1, False)


class Op:
    __slots__ = ("eng", "fn", "reads", "writes", "dma", "slot", "idx", "waits", "sig", "deps", "need", "label",
                 "cost", "alldeps", "fin", "nbytes")

    def __init__(self, eng, fn, reads, writes, dma, slot):
        self.eng = eng
        self.fn = fn
        self.reads = reads
        self.writes = writes
        self.dma = dma
        self.slot = slot
        self.waits = []
        self.sig = None
        self.deps = set()
        self.alldeps = set()
        self.need = False
        self.fin = 0.0
        self.nbytes = 0


class Sched:
    ENGS = ("pe", "act", "dve", "pool", "sp")
    WINDOW = {"pe": 16, "act": 6, "dve": 6, "pool": 6, "sp": 4}
    RAW_ONLY = False

    def __init__(self, nc):
        self.nc = nc
        self.ops = []
        self.label = ""
        self.reorder = True

    def add(self, eng, fn, reads=(), writes=(), dma=False, slot=None):
        op = Op(eng, fn, [r for r in (_region(a) for a in reads) if r],
                [r for r in (_region(a) for a in writes) if r], dma, slot)
        op.idx = len(self.ops)
        op.label = self.label
        n = 512
        aps = list(writes) + list(reads)
        if aps:
            n = aps[0].free_size()
        if dma:
            op.nbytes = max(a.free_size() * a.partition_size() * _DTSZ.get(a.dtype, 4) for a in aps)
            op.cost = 1200.0 if eng == "pool" else 100.0
        elif eng == "act":
            op.cost = (224 + n) / 1.2
        elif eng == "dve":
            op.cost = (64 + n) / 0.96
        elif eng == "pool":
            op.cost = 150 + 2.2 * n
        else:
            op.cost = 100.0
        self.ops.append(op)
        return op

    def _analyze(self):
        hist = {}
        ops = self.ops
        for op in ops:
            accs = [(r, False) for r in op.reads] + [(r, True) for r in op.writes]
            for (name, p0, p1, b0, b1, is_psum), w in accs:
                lst = hist.setdefault(name, [])
                for h in lst:
                    if h[5] == op.idx:
                        continue
                    if h[0] < p1 and p0 < h[1] and h[2] < b1 and b0 < h[3]:
                        if w or h[4] or (is_psum and h[6] != op.eng):
                            op.deps.add((h[5], h[4], w))
            for (name, p0, p1, b0, b1, is_psum), w in accs:
                lst = hist[name]
                if w:
                    lst[:] = [h for h in lst if not (p0 <= h[0] and h[1] <= p1 and b0 <= h[2] and h[3] <= b1)]
                lst.append([p0, p1, b0, b1, w, op.idx, op.eng, op.dma])
        for op in ops:
            real = set()
            for (j, jw, iw) in op.deps:
                d = ops[j]
                op.alldeps.add(j)
                if (not d.dma) and (not op.dma) and d.eng == op.eng:
                    if op.eng == "pe":
                        continue
                    if self.RAW_ONLY and not (jw and not iw):
                        continue
                real.add(j)
            op.deps = real

    def _schedule(self):
        ops = self.ops
        pend = {e: [o for o in ops if o.eng == e] for e in self.ENGS}
        if not self.reorder:
            return pend
        out = {e: [] for e in self.ENGS}
        free = {e: 0.0 for e in self.ENGS}
        done = [False] * len(ops)
        dma_free = [0.0]
        LAT = 300.0
        remaining = len(ops)
        while remaining:
            best = None
            for e in self.ENGS:
                lst = pend[e]
                if not lst:
                    continue
                W = self.WINDOW[e]
                cand = None
                for k in range(min(W, len(lst))):
                    o = lst[k]
                    ok = True
                    rdy = 0.0
                    for j in o.alldeps:
                        if not done[j]:
                            ok = False
                            break
                        d = ops[j]
                        t = d.fin + (0.0 if (d.eng == e and not d.dma) else LAT)
                        if t > rdy:
                            rdy = t
                    if not ok:
                        continue
                    st = max(free[e], rdy)
                    key = (st + 40.0 * k, o.idx)
                    if cand is None or key < cand[0]:
                        cand = (key, st, k, o)
                    if st <= free[e]:
                        break
                if cand is not None and (best is None or cand[0] < best[0]):
                    best = cand + (e,)
            key, st, k, o, e = best
            pend[e].pop(k)
            out[e].append(o)
            if o.dma:
                free[e] = st + o.cost
                t0 = max(st + o.cost, dma_free[0])
                dma_free[0] = t0 + o.nbytes / 250.0
                o.fin = dma_free[0] + 2000.0
            else:
                free[e] = st + o.cost
                o.fin = free[e]
            done[o.idx] = True
            remaining -= 1
        self.makespan = max(o.fin for o in ops)
        return out

    def build(self, final_wait_eng="sp"):
        nc = self.nc
        self._analyze()
        streams = self._schedule()
        ops = self.ops
        pos = {}
        for e in self.ENGS:
            for i, op in enumerate(streams[e]):
                pos[op.idx] = i
        for op in ops:
            last = {}
            keep = set()
            for j in op.deps:
                d = ops[j]
                if d.dma:
                    keep.add(j)
                elif d.eng not in last or pos[j] > pos[last[d.eng]]:
                    last[d.eng] = j
            keep.update(last.values())
            op.deps = keep
            for j in keep:
                ops[j].need = True
        sems = {}
        cnt = {}
        for op in ops:
            if op.dma:
                key = "d_" + op.slot
                cnt[key] = cnt.get(key, 0) + 16
                op.sig = (key, cnt[key], 16)
        for e in self.ENGS:
            for op in streams[e]:
                if (not op.dma) and op.need:
                    key = "e_" + op.eng
                    cnt[key] = cnt.get(key, 0) + 1
                    op.sig = (key, cnt[key], 1)
        for e in self.ENGS:
            sd = {}
            for op in streams[e]:
                w = {}
                for j in op.deps:
                    k, v, _ = ops[j].sig
                    if v > w.get(k, 0):
                        w[k] = v
                for k, v in w.items():
                    if sd.get(k, 0) < v:
                        sd[k] = v
                        op.waits.append((k, v))
        final = [(k, v) for k, v in cnt.items() if k.startswith("d_")]
        for k in cnt:
            sems[k] = nc.alloc_semaphore("s_" + k)

        def emit(eng_name, eng):
            for op in streams[eng_name]:
                for k, v in op.waits:
                    eng.wait_ge(sems[k], v)
                ins = op.fn(eng)
                if op.sig is not None:
                    ins.then_inc(sems[op.sig[0]], op.sig[2])
            if eng_name == final_wait_eng:
                for k, v in final:
                    eng.wait_ge(sems[k], v)

        with nc.Block() as block:
            @block.tensor
            def _(e):
                emit("pe", e)

            @block.scalar
            def _(e):
                emit("act", e)

            @block.vector
            def _(e):
                emit("dve", e)

            @block.gpsimd
            def _(e):
                emit("pool", e)

            @block.sync
            def _(e):
                emit("sp", e)
        self.streams = streams
        return {e: len(s) for e, s in streams.items()}, len(sems)


def build_program(debug=False):
    nc = bass.Bass("TRN2", target_bir_lowering=False)
    S = Sched(nc)
    dbg_n = [0]

    def dump(name, ap, shape, dt):
        if not debug:
            return
        o = nc.dram_tensor("dbg_" + name, list(shape), dt, kind="ExternalOutput").ap()
        dbg_n[0] += 1
        S.add("sp", lambda e: e.dma_start(out=o, in_=ap), reads=[ap], dma=True, slot="dbg%d" % dbg_n[0])

    def din(name, shape):
        return nc.dram_tensor(name, list(shape), F32, kind="ExternalInput").ap()

    xm = din("xm", [TOK, D])
    xp = din("xp", [TOK, D])
    w_in = din("w_in", [D, 3584])
    w_o = din("w_o", [D, D])
    w_gate = din("w_gate", [D, DFF])
    w_up = din("w_up", [D, DFF])
    w_down = din("w_down", [DFF, D])
    nrm = din("nrm", [3, D])
    gnw = din("gnw", [128, 4])
    wcv = din("wcv", [128, 12])
    rot_m = din("rot_m", [2, 128, TOK])
    rot_p = din("rot_p", [128, NT * 192])
    pww = din("pww", [128, NT * 4])
    cmat = din("cmat", [3, 128, 128])
    cmask = din("cmask", [128, 512])
    cxi = din("cxi", [128, 512])
    czeta = din("czeta", [128, 4])
    y = nc.dram_tensor("y", [TOK, D], F32, kind="ExternalOutput").ap()

    A = nc.alloc_sbuf_tensor
    x1 = A("x1", [128, NT, D], F32)
    x1b = x1[:, :, :].bitcast(BF16).rearrange("p t n -> p (t n)")
    WR = A("WR", [128, 40960], BF16)
    FR = A("FR", [128, 6664], F32)
    xs1 = A("xs1", [128, D], BF16)
    rt = A("rt", [128, 2, 2, 512], F32)
    gv = A("gv", [128, 2, D], F32)
    maskp = A("maskp", [128, 512], F32)
    xis = A("xis", [128, 512], F32)
    zeta = A("zeta", [128, 4], F32)
    pwt = A("pwt", [128, NT * 4], F32)
    gn = A("gn", [128, 4], F32)
    wc = A("wc", [128, 12], F32)
    nh = A("nh", [128, 512], F32)
    cm = A("cm", [128, 3, 128], BF16)
    stat = A("stat", [128, 3, 16], F32)
    st4 = A("st4", [128, 2, 16], F32)
    st4b = A("st4b", [128, 2, 16], BF16)
    cmf = A("cmf", [128, 2, 128], F32)
    Sst = A("Sst", [128, 512], F32)
    Sb = A("Sb", [128, 3, 512], BF16)
    sTm = A("sTm", [128, 2, 512], BF16)
    r2 = A("r2", [128, 512], BF16)
    junk = A("junk", [128, D], BF16)
    uh = A("uh", [128, 4, 2], F32)
    ps = [nc.alloc_psum_tensor("ps%d" % i, [128, 512], F32) for i in range(8)]
    ident = cm[:, 0, :]
    pswap = cm[:, 1, :]
    ones = cm[:, 2, :]

    def wr(off, n):
        return WR[:, off:off + n]
    ring = [wr(i * 4096, 4096).rearrange("p (k n) -> p k n", k=8) for i in range(3)]
    Wo = wr(12288, 8192).rearrange("p (k n) -> p k n", k=8)
    hT = wr(20480, 4096).rearrange("p (k n) -> p k n", k=8)
    qT = wr(24576, 2048).rearrange("p (h n) -> p h n", h=4)
    kT = wr(26624, 2048).rearrange("p (h n) -> p h n", h=4)
    kz = wr(28672, 2048).rearrange("p (t n) -> p t n", t=4)
    vv = wr(30720, 2048).rearrange("p (t n) -> p t n", t=4)
    sg = wr(32768, 2048).rearrange("p (h n) -> p h n", h=4)
    mixT = wr(34816, 4096).rearrange("p (k n) -> p k n", k=8)
    xs = [wr(38912, 1024)]
    qb = [wr(39936, 512), wr(40448, 512)]
    h2T = [wr(o, 4096).rearrange("p (k n) -> p k n", k=8) for o in (20480, 24576, 12288, 16384)]
    fring = [(wr(o, 4096).rearrange("p (k n) -> p k n", k=8),
              wr(o + 4096, 4096).rearrange("p (k n) -> p k n", k=8),
              wr(o + 8192, 4096).rearrange("p (c n) -> p c n", c=4)) for o in (0, 28672)]
    def xb(off, n):
        return x1b[:, off:off + n]
    Wk0 = wr(28672, 4096).rearrange("p (k n) -> p k n", k=8)
    Wv0 = wr(32768, 4096).rearrange("p (k n) -> p k n", k=8)
    Wcc0 = xb(8192, 4096).rearrange("p (k n) -> p k n", k=8)
    Wch0 = xb(12288, 4096).rearrange("p (k n) -> p k n", k=8)
    kw = xb(16384, 8192).rearrange("p (t n) -> p t n", t=16)
    vpre = xb(24576, 8192).rearrange("p (t n) -> p t n", t=16)

    def fr(off, n):
        return FR[:, off:off + n]
    ra = [fr(0, 512), fr(512, 512)]
    rb = [fr(1024, 512), fr(1536, 512)]
    ccs = [fr(2048, 512), fr(2560, 512)]
    ub = [fr(3072, 514), fr(3586, 514)]
    yb = [fr(4100, 512), fr(4612, 512)]
    msq = fr(5124, 512)
    retb = [fr(5636, 512), fr(6148, 512)]
    stg = [fr(2048, 1024), fr(3072, 1024),
           WR[:, 24576:26624].bitcast(F32), WR[:, 36864:38912].bitcast(F32)]
    sgt = [fr(0, 512), fr(512, 512)]
    ost = [fr(1024, 1024), fr(2048, 1024), fr(5632, 1024)]
    xs2 = FR[:, 5120:5632].bitcast(BF16)
    XS01 = [xs[0], xs1[:, :]]
    XS2 = [xs2, xs1[:, :]]
    actT = [FR[:, 3072:4096].bitcast(BF16).rearrange("p (c n) -> p c n", c=4),
            FR[:, 4096:5120].bitcast(BF16).rearrange("p (c n) -> p c n", c=4)]

    CDEC = [float(np.float32(np.exp(128.0 * np.log1p(-np.exp2(-5.0 - h))))) for h in range(4)]
    bank_i = [0]

    def nb():
        b = ps[bank_i[0] % 8]
        bank_i[0] += 1
        return b

    def dma(q, out, in_, slot):
        S.add(q, lambda e: e.dma_start(out=out, in_=in_), reads=[in_], writes=[out], dma=True, slot=slot)

    def mm(out, lhsT, rhs, start, stop):
        op = S.add("pe", lambda e: e.matmul(out, lhsT, rhs, start=start, stop=stop), reads=[lhsT, rhs], writes=[out])
        op.cost = max(100.0, rhs.free_size() / 2.33) * (4.0 if rhs.dtype == F32 else 1.0)

    def tr(out, in_):
        op = S.add("pe", lambda e: e.transpose(out, in_, ident), reads=[in_, ident], writes=[out])
        op.cost = 100.0

    def act(out, in_, func, scale=1.0, bias=0.0, accum=None):
        if accum is None:
            S.add("act", lambda e: e.activation(out=out, in_=in_, func=func, scale=scale, bias=bias),
                  reads=[in_], writes=[out])
        else:
            S.add("act", lambda e: e.activation(out=out, in_=in_, func=func, accum_out=accum),
                  reads=[in_], writes=[out, accum])

    def acopy(out, in_):
        S.add("act", lambda e: e.copy(out, in_), reads=[in_], writes=[out])

    def tt(eng, out, a, b, op):
        assert out.free_size() == a.free_size() == b.free_size(), (out.shape, a.shape, b.shape, repr(out), repr(a), repr(b))
        S.add(eng, lambda e: e.tensor_tensor(out, a, b, op), reads=[a, b], writes=[out])

    def tsc(eng, out, a, s1, s2, op0, op1):
        rd = [a] + [s for s in (s1, s2) if not isinstance(s, (int, float)) and s is not None]
        S.add(eng, lambda e: e.tensor_scalar(out, a, s1, s2, op0, op1), reads=rd, writes=[out])

    def stt(eng, out, in0, sc, in1, op0, op1):
        rd = [in0, in1] + ([] if isinstance(sc, (int, float)) else [sc])
        S.add(eng, lambda e: e.scalar_tensor_tensor(out, in0, sc, in1, op0, op1), reads=rd, writes=[out])

    def cp(eng, out, in_):
        S.add(eng, lambda e: e.tensor_copy(out, in_), reads=[in_], writes=[out])

    dma("pool", cm[:, :, :], cmat.rearrange("c p n -> p c n"), "cm")
    S.add("pool", lambda e: e.memset(nh[:, :], -0.5), writes=[nh[:, :]])

    def late_consts():
        dma("sp", pwt[:, :], pww[:, :], "c4")
        dma("sp", zeta[:, :], czeta[:, :], "c3")
        dma("sp", maskp[:, :], cmask[:, :], "c0")
        dma("sp", xis[:, :], cxi[:, :], "c1")
        dma("sp", gn[:, :], gnw[:, :], "c5")
        dma("sp", wc[:, :], wcv[:, :], "c6")
        dma("sp", gv[:, 1, :], nrm[1:2, :].partition_broadcast(128), "gv1")
        dma("sp", cmf[:, 0, :], cmat[0, :, :], "cmf0")
        dma("sp", cmf[:, 1, :], cmat[2, :, :], "cmf1")
        S.add("pool", lambda e: e.memset(uh[:, :, :], 0.0), writes=[uh[:, :, :]])

    def wblock(dst, col0, slot):
        dma("pool", dst, w_in[:, col0:col0 + 512].rearrange("(k p) n -> p k n", p=128), slot)

    wblock(Wk0, 512, "wk0")
    wblock(Wv0, 1024, "wv0")

    stat_i = [0]

    def norm_A(src, gidx, xsb):
        S.label = 'normA'
        c = stat_i[0] % 16
        stat_i[0] += 1
        ss = stat[:, 0, c:c + 1]
        ms = stat[:, 1, c:c + 1]
        rs = stat[:, 2, c:c + 1]
        act(junk[:, :], src, AF.Square, accum=ss)
        tsc("pool", ms, ss, 1.0 / D, EPS, ALU.mult, ALU.add)
        tt("pool", rs, ms, nh[:, 0:1], ALU.pow)
        stt("dve", xsb, src, rs, gv[:, gidx, :], ALU.mult, ALU.mult)

    def norm_B(xsb, dst):
        S.label = 'normB'
        b = nb()
        bb = b[:, :].bitcast(BF16)
        for kc in range(8):
            tr(bb[:, kc * 128:(kc + 1) * 128], xsb[:, kc * 128:(kc + 1) * 128])
        acopy(dst, bb.rearrange("p (k n) -> p k n", k=8))

    def norm_pipe(tiles, slots, fillers=(), pre=None, skipA=0):
        fl = list(fillers)
        n = len(tiles)
        for i in range(skipA, min(2, n)):
            if pre is not None:
                pre(i)
            norm_A(tiles[i][0], tiles[i][1], slots[i % 2])
        for i in range(n):
            if fl:
                fl.pop(0)()
            norm_B(slots[i % 2], tiles[i][2])
            if i + 2 < n:
                if pre is not None:
                    pre(i + 2)
                norm_A(tiles[i + 2][0], tiles[i + 2][1], slots[i % 2])
        while fl:
            fl.pop(0)()

    rot_i = [0]

    def proj_fm(W, c, rhs_hT, n=512):
        b = nb()
        for kc in range(8):
            mm(b[:, 0:n], W[:, kc, c * 128:(c + 1) * 128], rhs_hT[:, kc, :], kc == 0, kc == 7)
        return b

    def rotary_a(b, rslot):
        i = rot_i[0] % 2
        rot_i[0] += 1
        acopy(qb[i], b[:, :])
        tt("dve", ra[i], b[:, :], rt[:, rslot, 0, :], ALU.mult)
        return i

    def rotary_b(i, rslot, out, xi):
        b2 = nb()
        mm(b2[:, :], pswap, qb[i], True, True)
        tt("dve", rb[i], b2[:, :], rt[:, rslot, 1, :], ALU.mult)
        if xi is None:
            tt("pool", out, ra[i], rb[i], ALU.add)
        else:
            tt("pool", ra[i], ra[i], rb[i], ALU.add)
            tt("pool", out.rearrange("p (c i) -> p c i", i=128), ra[i].rearrange("p (c i) -> p c i", i=128),
               xi.unsqueeze(1).to_broadcast([128, 4, 128]), ALU.mult)

    def proj_rot(W, dstT, rslot, with_xi, hsrc=None):
        S.label = 'projrot'
        hsrc = hT if hsrc is None else hsrc
        pend = None
        for hd in range(4):
            b = proj_fm(W, hd, hsrc)
            st = rotary_a(b, rslot)
            if pend is not None:
                rotary_b(*pend)
            pend = (st, rslot, dstT[:, hd, :], xis[:, hd * 128:(hd + 1) * 128] if with_xi else None)
        rotary_b(*pend)

    def load_rt(tab, g, slot):
        dma("sp", rt[:, slot, :, :], tab[:, :, g * 512:(g + 1) * 512].rearrange("c p n -> p c n"), "rt%d" % slot)

    rt_n = [0]
    blk_cols = [512, 0, 1024, 1536, 2560, 3072, 2048]
    ring_n = [0]
    pending_blocks = []

    def issue_block(col0):
        s = ring_n[0] % 3
        ring_n[0] += 1
        wblock(ring[s], col0, "ring%d" % s)
        return ring[s]

    def load_x_group(g):
        for tl in range(4):
            t = g * 4 + tl
            dma("sp", x1[:, t, :], xm[t * 128:(t + 1) * 128, :], "x%d" % t)

    def x_tiles(g, gidx, dstT):
        return [(x1[:, g * 4 + tl, :], gidx, dstT[:, :, tl * 128:(tl + 1) * 128]) for tl in range(4)]

    def ktr0(pg, tl):
        def f():
            S.label = 'ktr0'
            t = pg * 4 + tl
            b = nb()
            bb = b[:, :].bitcast(BF16)
            for hd in range(4):
                tr(bb[:, hd * 128:(hd + 1) * 128], kT[:, hd, tl * 128:(tl + 1) * 128])
            tt("dve", kw[:, t, :].rearrange("p (h d) -> p h d", h=4),
               bb[:, 0:512].rearrange("p (h d) -> p h d", h=4),
               pwt[:, t * 4:(t + 1) * 4].unsqueeze(2).to_broadcast([128, 4, 128]), ALU.mult)
        return f

    hTp = [hT, wr(12288, 4096).rearrange("p (k n) -> p k n", k=8)]
    rtf = [rt[:, sl, :, :].rearrange("p a n -> p (a n)") for sl in range(2)]

    def prefix_tile(pg, tl, hcur, rslot):
        t = pg * 4 + tl
        S.label = 'k0'
        i = rot_i[0] % 2
        rot_i[0] += 1
        bk = nb()
        for kc in range(8):
            mm(bk[:, :], hcur[:, kc, tl * 128:(tl + 1) * 128], Wk0[:, kc, :], kc == 0, kc == 7)
        k4 = bk[:, :].rearrange("p (h t f) -> p h t f", h=4, t=2)
        tab = rtf[rslot][:, tl * 192:(tl + 1) * 192]
        cosb = tab[:, 0:64].unsqueeze(1).unsqueeze(1).to_broadcast([128, 4, 2, 64])
        sinb = tab[:, 64:128].unsqueeze(1).to_broadcast([128, 4, 64])
        nsinb = tab[:, 128:192].unsqueeze(1).to_broadcast([128, 4, 64])
        a4 = ra[i].rearrange("p (h t f) -> p h t f", h=4, t=2)
        b4 = rb[i].rearrange("p (h t f) -> p h t f", h=4, t=2)
        tt("dve", a4, k4, cosb, ALU.mult)
        tt("dve", b4[:, :, 0, :], k4[:, :, 1, :], nsinb, ALU.mult)
        tt("dve", b4[:, :, 1, :], k4[:, :, 0, :], sinb, ALU.mult)
        tt("pool", ra[i], ra[i], rb[i], ALU.add)
        tt("pool", kw[:, t, :].rearrange("p (h d) -> p h d", h=4), ra[i].rearrange("p (h d) -> p h d", h=4),
           pwt[:, t * 4:(t + 1) * 4].unsqueeze(2).to_broadcast([128, 4, 128]), ALU.mult)
        S.label = 'v0'
        bv = nb()
        for kc in range(8):
            mm(bv[:, :], hcur[:, kc, tl * 128:(tl + 1) * 128], Wv0[:, kc, :], kc == 0, kc == 7)
        acopy(vpre[:, t, :], bv[:, :])

    def load_rt0(pg, slot):
        dma("sp", rtf[slot][:, 0:768], rot_p[:, pg * 768:(pg + 1) * 768], "rt%d" % slot)

    for pg in range(NG):
        rslot = rt_n[0] % 2
        rt_n[0] += 1
        hcur = hTp[pg % 2]
        if pg == 0:
            dma("act", gv[:, 0, :], nrm[0:1, :].partition_broadcast(128), "gv0")
            for tl in range(4):
                dma("sp", x1[:, tl, :], xp[tl * 128:(tl + 1) * 128, :], "x%d" % tl)
            load_rt0(0, 0)
            srcs = [x1[:, tl, :] for tl in range(4)]
            norm_A(srcs[0], 0, XS01[0])
            norm_A(srcs[1], 0, XS01[1])
        else:
            load_rt0(pg, rslot)
            srcs = [stg[tl] for tl in range(4)]
        dsts = [hcur[:, :, tl * 128:(tl + 1) * 128] for tl in range(4)]
        for tl in range(4):
            norm_B(XS01[tl % 2], dsts[tl])
            if tl + 2 < 4:
                norm_A(srcs[tl + 2], 0, XS01[tl % 2])
            if tl == 0 and pg == 0:
                late_consts()
            if tl >= 1:
                prefix_tile(pg, tl - 1, hcur, rslot)
            if tl == 1 and pg + 1 < NG:
                for i in range(4):
                    t = (pg + 1) * 4 + i
                    dma("sp", stg[i], xp[t * 128:(t + 1) * 128, :], "stg%d" % i)
        if pg + 1 < NG:
            for i in range(2):
                norm_A(stg[i], 0, XS01[i])
        prefix_tile(pg, 3, hcur, rslot)
        if pg == 1:
            wblock(Wcc0, 2560, "wcc0")
            wblock(Wch0, 3072, "wch0")
            load_x_group(0)
        if pg == 2:
            blocks = [issue_block(blk_cols[0]), issue_block(blk_cols[1]), issue_block(blk_cols[2])]
    hlast = hTp[(NG - 1) % 2]
    for i in range(2):
        norm_A(x1[:, i, :], 0, XS01[i])
    S.label = 'halo'
    for c in range(4):
        b1 = nb()
        for kc in range(8):
            mm(b1[:, 0:2], Wcc0[:, kc, c * 128:(c + 1) * 128], hlast[:, kc, 510:512], kc == 0, kc == 7)
        b2 = nb()
        for kc in range(8):
            mm(b2[:, 0:2], Wch0[:, kc, c * 128:(c + 1) * 128], hlast[:, kc, 510:512], kc == 0, kc == 7)
        acopy(ccs[0][:, 0:2], b1[:, 0:2])
        tt("dve", uh[:, c, :], b2[:, 0:2], ccs[0][:, 0:2], ALU.mult)
    S.label = 'S0'
    b = nb()
    for hd in range(4):
        for t in range(NT):
            mm(b[:, hd * 128:(hd + 1) * 128], kw[:, t, hd * 128:(hd + 1) * 128],
               vpre[:, t, hd * 128:(hd + 1) * 128], t == 0, t == NT - 1)
    cp("dve", Sst[:, :], b[:, :])
    acopy(Sb[:, 0, :], b[:, :])
    dump("S0", Sst[:, :], [128, 512], F32)
    dump("uh0", uh[:, :, :], [128, 4, 2], F32)

    dma("pool", Wo, w_o.rearrange("(k p) n -> p k n", p=128), "wo")
    norm_pipe(x_tiles(0, 0, hT), XS01, skipA=2)
    fgs = [(0, 384), (384, 512), (896, 512), (1408, 512), (1920, 512), (2432, 384)]
    NFG = len(fgs)

    def load_fg(fi):
        f0, w = fgs[fi]
        s = fi % 2
        Wg_, Wu_, Wd_ = fring[s]
        dma("pool", Wg_[:, :, 0:w], w_gate[:, f0:f0 + w].rearrange("(k p) n -> p k n", p=128), "fg%d" % s)
        dma("pool", Wu_[:, :, 0:w], w_up[:, f0:f0 + w].rearrange("(k p) n -> p k n", p=128), "fu%d" % s)
        dma("pool", Wd_[:, 0:w // 128, :], w_down[f0:f0 + w, :].rearrange("(c p) n -> p c n", p=128), "fd%d" % s)

    chunk_par = [0]

    for g in range(NG):
        rslot = rt_n[0] % 2
        rt_n[0] += 1
        load_rt(rot_m, g, rslot)
        if g + 1 < NG:
            load_x_group(g + 1)
        Wk_, Wq_, Wv_ = blocks[0], blocks[1], blocks[2]
        proj_rot(Wk_, kT, rslot, False)
        Wg_ = issue_block(blk_cols[3])
        proj_rot(Wq_, qT, rslot, True)
        Wcc_ = issue_block(blk_cols[4])
        S.label = 'ktr'
        for tl in range(4):
            b = nb()
            bb = b[:, :].bitcast(BF16)
            for hd in range(4):
                tr(bb[:, hd * 128:(hd + 1) * 128], kT[:, hd, tl * 128:(tl + 1) * 128])
            tt("dve", kz[:, tl, :].rearrange("p (h d) -> p h d", h=4),
               bb[:, 0:512].rearrange("p (h d) -> p h d", h=4),
               zeta[:, :].unsqueeze(2).to_broadcast([128, 4, 128]), ALU.mult)

        S.label = 'v'
        for tl in range(4):
            b = nb()
            for kc in range(8):
                mm(b[:, :], hT[:, kc, tl * 128:(tl + 1) * 128], Wv_[:, kc, :], kc == 0, kc == 7)
            acopy(vv[:, tl, :], b[:, :])
        Wch_ = issue_block(blk_cols[5])
        def ret_front(tl):
            S.label = 'front'
            c0 = tl * 128
            bs = nb()
            for hd in range(4):
                mm(bs[:, hd * 128:(hd + 1) * 128], kT[:, hd, c0:c0 + 128], qT[:, hd, c0:c0 + 128], True, True)
            sl = tl % 2
            tt("dve", sTm[:, sl, :], bs[:, :], maskp[:, :], ALU.mult)
            bi = nb()
            for hd in range(4):
                mm(bi[:, hd * 128:(hd + 1) * 128], kz[:, tl, hd * 128:(hd + 1) * 128],
                   vv[:, tl, hd * 128:(hd + 1) * 128], True, True)
            n = chunk_par[0]
            chunk_par[0] += 1
            for hd in range(4):
                hs = slice(hd * 128, (hd + 1) * 128)
                stt("dve", Sst[:, hs], Sst[:, hs], CDEC[hd], bi[:, hs], ALU.mult, ALU.add)
            acopy(Sb[:, (n + 1) % 3, :], Sst[:, :])

        def ret_back(tl, n0):
            S.label = 'back'
            c0 = tl * 128
            sl = tl % 2
            br = nb()
            for hd in range(4):
                hs = slice(hd * 128, (hd + 1) * 128)
                mm(br[:, hs], vv[:, tl, hs], sTm[:, sl, hs], True, False)
                mm(br[:, hs], Sb[:, n0 % 3, hs], qT[:, hd, c0:c0 + 128], False, True)
            act(r2[:, :], br[:, :], AF.Square)
            ret3 = retb[tl % 2].rearrange("p (h i) -> p h i", h=4)
            tt("dve", ret3, br[:, :].rearrange("p (h i) -> p h i", h=4),
               gn[:, :].unsqueeze(2).to_broadcast([128, 4, 128]), ALU.mult)
            tt("pool", ret3, ret3, sg[:, :, c0:c0 + 128], ALU.mult)

        def ret_nA(tl):
            S.label = 'rnormA'
            bq = nb()
            for hd in range(4):
                mm(bq[:, hd:hd + 1], r2[:, hd * 128:(hd + 1) * 128], ones[:, 0:1], True, True)
            c = stat_i[0] % 4
            stat_i[0] += 1
            ms4 = st4[:, 0, c * 4:(c + 1) * 4]
            rs4 = st4[:, 1, c * 4:(c + 1) * 4]
            tsc("dve", ms4, bq[:, 0:4], 1.0 / 128, EPS, ALU.mult, ALU.add)
            tt("pool", rs4, ms4, nh[:, 0:4], ALU.pow)
            hi = st4b[:, 0, c * 4:(c + 1) * 4]
            lo = st4b[:, 1, c * 4:(c + 1) * 4]
            cp("pool", hi, rs4)
            tt("pool", lo, rs4, hi, ALU.subtract)
            return (hi, lo)

        def ret_nB(tl, hl):
            c0 = tl * 128
            hi, lo = hl
            bq2 = nb()
            S.label = 'rnormH'
            for hd in range(4):
                mm(bq2[:, hd * 128:(hd + 1) * 128], hi[:, hd:hd + 1].to_broadcast([128, 128]), ident, True, False)
                mm(bq2[:, hd * 128:(hd + 1) * 128], lo[:, hd:hd + 1].to_broadcast([128, 128]), ident, False, True)
            S.label = 'rnormB'
            tt("dve", mixT[:, 0:4, c0:c0 + 128], retb[tl % 2].rearrange("p (h i) -> p h i", h=4),
               bq2[:, :].rearrange("p (h i) -> p h i", h=4), ALU.mult)

        def g_chunk(c):
            S.label = 'g'
            b = proj_fm(Wg_, c, hT)
            act(sg[:, c, :], b[:, :], AF.Silu)

        conv_i = [0]

        def conv_a(c):
            S.label = 'conv'
            i = c % 2
            b1 = proj_fm(Wcc_, c, hT)
            acopy(ccs[i], b1[:, :])
            b2 = proj_fm(Wch_, c, hT)
            u = ub[i]
            cp("pool", u[:, 0:2], uh[:, c, :])
            tt("dve", u[:, 2:514], b2[:, :], ccs[i], ALU.mult)
            cp("pool", uh[:, c, :], u[:, 512:514])
            yy = yb[i]
            w0 = wc[:, c:c + 1]
            S.add("act", lambda e: e.activation(out=yy, in_=u[:, 0:512], func=AF.Copy, scale=w0),
                  reads=[u[:, 0:512], w0], writes=[yy])
            stt("dve", yy, u[:, 1:513], wc[:, 4 + c:5 + c], yy, ALU.mult, ALU.add)
            stt("dve", yy, u[:, 2:514], wc[:, 8 + c:9 + c], yy, ALU.mult, ALU.add)

        def conv_b(c, Wcb_):
            S.label = 'conv'
            b3 = proj_fm(Wcb_, c, hT)
            tt("dve", mixT[:, 4 + c, :], b3[:, :], yb[c % 2], ALU.mult)

        n_base = chunk_par[0]
        if g + 1 < NG:
            nxt_tiles, nxt_slots = x_tiles(g + 1, 0, hT), XS01
        else:
            nxt_tiles, nxt_slots = x_tiles(0, 1, h2T[0]), XS01
        ret_front(0)
        g_chunk(0)
        g_chunk(1)
        g_chunk(2)
        g_chunk(3)
        Wcb_ = issue_block(blk_cols[6])
        ret_back(0, n_base)
        ret_front(1)
        conv_a(0)
        h0 = ret_nA(0)
        ret_back(1, n_base + 1)
        ret_front(2)
        conv_a(1)
        ret_nB(0, h0)
        h1 = ret_nA(1)
        conv_b(0, Wcb_)
        ret_back(2, n_base + 2)
        ret_front(3)
        conv_a(2)
        ret_nB(1, h1)
        h2_ = ret_nA(2)
        conv_b(1, Wcb_)
        ret_back(3, n_base + 3)
        norm_A(nxt_tiles[0][0], nxt_tiles[0][1], nxt_slots[0])
        conv_a(3)
        ret_nB(2, h2_)
        h3 = ret_nA(3)
        conv_b(2, Wcb_)
        norm_A(nxt_tiles[1][0], nxt_tiles[1][1], nxt_slots[1])
        conv_b(3, Wcb_)
        if g == 0:
            dump("hT", WR[:, 20480:24576], [128, 4096], BF16)
            dump("qT", WR[:, 24576:26624], [128, 2048], BF16)
            dump("kT", WR[:, 26624:28672], [128, 2048], BF16)
            dump("kz", WR[:, 28672:30720], [128, 2048], BF16)
            dump("vv", WR[:, 30720:32768], [128, 2048], BF16)
            dump("sg", WR[:, 32768:34816], [128, 2048], BF16)
            dump("mixT", WR[:, 34816:38912], [128, 4096], BF16)
            dump("Sst", Sst[:, :], [128, 512], F32)
        def wo_tile(tl, g=g):
            def f():
                S.label = 'wo'
                t = g * 4 + tl
                for cb in range(2):
                    b = nb()
                    for kc in range(8):
                        mm(b[:, :], mixT[:, kc, tl * 128:(tl + 1) * 128], Wo[:, kc, cb * 512:(cb + 1) * 512], kc == 0, kc == 7)
                    tt("dve", x1[:, t, cb * 512:(cb + 1) * 512], x1[:, t, cb * 512:(cb + 1) * 512], b[:, :], ALU.add)
            return f
        fill = [wo_tile(tl) for tl in range(4)]
        f0 = fill[0]
        fill[0] = lambda f0=f0, h3=h3: (f0(), ret_nB(3, h3))
        if g + 1 < NG:
            blocks = [issue_block(blk_cols[0]), issue_block(blk_cols[1]), issue_block(blk_cols[2])]
            norm_pipe(nxt_tiles, nxt_slots, fill, skipA=2)
        else:
            load_fg(0)
            norm_pipe(nxt_tiles, nxt_slots, fill, skipA=2)

    dump("x1", x1[:, :, :], [128, NT, D], F32)
    steps = [(fi, tg) for fi in range(NFG) for tg in range(NG)]
    gu_i = [0]

    def gate_up(fi, tg, ai):
        f0, w = fgs[fi]
        Wg_, Wu_, Wd_ = fring[fi % 2]
        rhs = h2T[tg]

        def chunk(fc):
            def f():
                S.label = 'gu'
                bg = proj_fm(Wg_, fc, rhs)
                bu = proj_fm(Wu_, fc, rhs)
                i = gu_i[0] % 2
                gu_i[0] += 1
                act(sgt[i], bg[:, :], AF.Silu)
                tt("dve", actT[ai][:, fc, :], bu[:, :], sgt[i], ALU.mult)
            return f
        fill = [chunk(fc) for fc in range(w // 128)]
        if fi == 0 and tg + 1 < NG:
            norm_pipe(x_tiles(tg + 1, 1, h2T[tg + 1]), XS2, fill)
        else:
            for f in fill:
                f()

    fin_i = [0]

    def down(fi, tg, ai):
        S.label = 'down'
        f0, w = fgs[fi]
        Wg_, Wu_, Wd_ = fring[fi % 2]
        nfc = w // 128
        for tl in range(4):
            t = tg * 4 + tl
            for cb in range(2):
                b = nb()
                for fc in range(nfc):
                    mm(b[:, :], actT[ai][:, fc, tl * 128:(tl + 1) * 128], Wd_[:, fc, cb * 512:(cb + 1) * 512],
                       fc == 0, fc == nfc - 1)
                tt("dve", x1[:, t, cb * 512:(cb + 1) * 512], x1[:, t, cb * 512:(cb + 1) * 512], b[:, :], ALU.add)
            if fi == NFG - 1:
                c = stat_i[0] % 16
                stat_i[0] += 1
                ss = stat[:, 0, c:c + 1]
                ms = stat[:, 1, c:c + 1]
                rs = stat[:, 2, c:c + 1]
                oi = fin_i[0] % 3
                o = ost[oi]
                fin_i[0] += 1
                act(junk[:, :], x1[:, t, :], AF.Square, accum=ss)
                tsc("pool", ms, ss, 1.0 / D, EPS, ALU.mult, ALU.add)
                tt("pool", rs, ms, nh[:, 0:1], ALU.pow)
                xin = x1[:, t, :]
                if t % 2 == 0 or tg == NG - 1:
                    stt("dve", o, xin, rs, gv[:, 0, :], ALU.mult, ALU.mult)
                else:
                    S.add("act", lambda e, o=o, xin=xin, rs=rs: e.activation(out=o, in_=xin, func=AF.Copy, scale=rs),
                          reads=[xin, rs], writes=[o])
                    tt("pool", o, o, gv[:, 0, :], ALU.mult)
                dma("sp", y[t * 128:(t + 1) * 128, :], o, "out%d" % oi)

    dma("sp", gv[:, 0, :], nrm[2:3, :].partition_broadcast(128), "gv0")
    load_fg(1)
    loaded = 2
    for si, (fi, tg) in enumerate(steps):
        gate_up(fi, tg, si % 2)
        if si > 0:
            pfi, ptg = steps[si - 1]
            down(pfi, ptg, (si - 1) % 2)
            if ptg == NG - 1 and loaded < NFG:
                load_fg(loaded)
                loaded += 1
    pfi, ptg = steps[-1]
    down(pfi, ptg, (len(steps) - 1) % 2)

    info = S.build()
    global _LAST_SCHED
    _LAST_SCHED = S
    return nc, info


def _tables():
    h = np.arange(4, dtype=np.float64)
    log_g = np.log1p(-np.exp2(-5.0 - h))
    i128 = np.arange(128)
    jj = i128[:, None]
    ii = i128[None, :]
    same = (jj // 64) == (ii // 64)
    lower = (jj < 64) & (ii >= 64)
    mask = np.zeros((128, 4, 128), np.float64)
    for hd in range(4):
        e_same = np.abs(ii - jj) - (ii + 1.0)
        e_low = (ii - jj) - (ii + 1.0)
        mask[:, hd, :] = np.where(same, np.exp(log_g[hd] * e_same), np.where(lower, np.exp(log_g[hd] * e_low), 0.0))
    xi = np.exp(log_g[:, None] * (i128[None, :] + 1.0)) * (128.0 ** -0.5)
    xi = np.broadcast_to(xi.reshape(1, 512), (128, 512))
    zeta = np.exp(log_g[None, :] * (127.0 - i128[:, None]))
    ident = np.eye(128)
    pswap = np.zeros((128, 128))
    for m in range(128):
        pswap[(m + 64) % 128, m] = 1.0
    cmat = np.stack([ident, pswap, np.ones((128, 128))])
    f32 = np.float32
    return dict(cmask=mask.reshape(128, 512).astype(f32), cxi=np.ascontiguousarray(xi).astype(f32),
                czeta=zeta.astype(f32), cmat=cmat.astype(f32)), log_g


def _rot_table(pos0):
    d = 128
    expo = (np.arange(0, d, 2, dtype=np.float32) / np.float32(d)).astype(np.float64)
    inv_freq = (np.float32(1.0) / (10000.0 ** expo).astype(np.float32)).astype(np.float32)
    pos = np.arange(pos0, pos0 + TOK, dtype=np.float32)
    ang = (pos[None, :] * inv_freq[:, None]).astype(np.float32)
    c = np.cos(ang.astype(np.float64))
    s = np.sin(ang.astype(np.float64))
    cosT = np.concatenate([c, c], axis=0)
    sinT = np.concatenate([-s, s], axis=0)
    return np.stack([cosT, sinT]).astype(np.float32)


_CACHE = {}
_LAST_SCHED = None


def kernel(x, norm1_w, w_in, w_conv, ret_gn_w, w_o, norm2_w, w_gate, w_up, w_down, final_norm_w):
    return _run(x, norm1_w, w_in, w_conv, ret_gn_w, w_o, norm2_w, w_gate, w_up, w_down, final_norm_w)[0]


def _run(x, norm1_w, w_in, w_conv, ret_gn_w, w_o, norm2_w, w_gate, w_up, w_down, final_norm_w, debug=False):
    f32 = np.float32
    x = np.asarray(x, f32)
    key = "nc_dbg" if debug else "nc"
    if key not in _CACHE:
        _CACHE[key] = build_program(debug)
    nc, info = _CACHE[key]
    consts, log_g = _tables()
    nrm = np.stack([np.asarray(norm1_w, f32).reshape(D), np.asarray(norm2_w, f32).reshape(D),
                    np.asarray(final_norm_w, f32).reshape(D)])
    gnw = np.ascontiguousarray(np.asarray(ret_gn_w, f32).reshape(4, 128).T)
    wcv = np.ascontiguousarray(np.asarray(w_conv, f32).reshape(3, 4, 128).transpose(2, 0, 1).reshape(128, 12))
    shared = dict(
        w_in=np.ascontiguousarray(np.asarray(w_in, f32).reshape(D, 3584)),
        w_o=np.ascontiguousarray(np.asarray(w_o, f32).reshape(D, D)),
        w_gate=np.ascontiguousarray(np.asarray(w_gate, f32).reshape(D, DFF)),
        w_up=np.ascontiguousarray(np.asarray(w_up, f32).reshape(D, DFF)),
        w_down=np.ascontiguousarray(np.asarray(w_down, f32).reshape(DFF, D)),
        nrm=nrm, gnw=gnw, wcv=wcv, **consts)
    rots = [_rot_table(0), _rot_table(TOK)]
    c0 = rots[0][0, 0:64, :].T.reshape(NT, 128, 64)
    s0 = rots[0][1, 64:128, :].T.reshape(NT, 128, 64)
    rot_tm = np.ascontiguousarray(np.concatenate([c0, s0, -s0], axis=2).transpose(1, 0, 2).reshape(128, NT * 192))
    j = np.arange(TOK, dtype=np.float64)
    pw1 = np.exp(log_g[None, :] * (TOK - 1.0 - j[:, None]))
    pw1 = pw1.reshape(NT, 128, 4).transpose(1, 0, 2).reshape(128, NT * 4).astype(f32)
    pw0 = np.zeros_like(pw1)
    zeros_x = np.zeros((TOK, D), f32)
    in_maps = []
    for c in range(NCORES):
        b, half = divmod(c, 2)
        m = dict(shared)
        m["xm"] = np.ascontiguousarray(x[b, half * TOK:(half + 1) * TOK])
        m["xp"] = np.ascontiguousarray(x[b, 0:TOK]) if half == 1 else zeros_x
        m["rot_m"] = rots[half]
        m["rot_p"] = rot_tm
        m["pww"] = pw1 if half == 1 else pw0
        in_maps.append(m)
    res = run_bass_kernel_spmd(nc, in_maps, core_ids=list(range(NCORES)))
    out = np.empty((NB, SEQ, D), f32)
    for c in range(NCORES):
        b, half = divmod(c, 2)
        out[b, half * TOK:(half + 1) * TOK] = res.results[c]["y"]
    return out, res
```
